# Optimizing a Trainium2 kernel written in Bass

```python
import math
import jax, jax.numpy as jnp
from jax import lax
import numpy as np

D_MODEL = 4096
BATCH = 8
SEQ = 2048
DEPTH = 4
DEC_BATCH = 16
DEC_SEQ = 32
PAST_LEN = 2048

CHUNK = 64
N_MIXERS = 3
N_S5_LAYERS = (DEPTH + 2) // 3
N_ATTN_LAYERS = (DEPTH + 1) // 3
N_HGRN_LAYERS = DEPTH // 3
S5_GROUP = 16
S5_GROUPS = D_MODEL // S5_GROUP
S5_STATE = 64
ATTN_HEADS = 32
ATTN_HEAD_DIM = D_MODEL // ATTN_HEADS
BAND_CHUNKS = 8
WINDOW = BAND_CHUNKS * CHUNK
REL_CLIP = 128
HGRN_HEADS = 32
HGRN_EXPAND = 128
HGRN_FDIM = HGRN_HEADS * HGRN_EXPAND
HGRN_VDIM = D_MODEL // HGRN_HEADS
HGRN_BLOCK = 16
D_FF = -(-8 * D_MODEL // (3 * 256)) * 256
RMS_EPS = 1e-6

kernel_name = 'hybrid_streaming_s5_chunkattn_hgrn2_step'


def rms_norm(x, gain):
    x32 = x.astype(jnp.float32)
    y = x32 * lax.rsqrt(jnp.mean(x32 * x32, axis=-1, keepdims=True) + RMS_EPS)
    return (y * gain.astype(jnp.float32)).astype(x.dtype)


def swiglu(h, w_gate_up, w_down):
    gate, up = jnp.split(h @ w_gate_up, 2, axis=-1)
    return (jax.nn.silu(gate) * up) @ w_down


def _complex_affine_combine(left, right):
    a1r, a1i, b1r, b1i = left
    a2r, a2i, b2r, b2i = right
    return (a2r * a1r - a2i * a1i,
            a2r * a1i + a2i * a1r,
            a2r * b1r - a2i * b1i + b2r,
            a2r * b1i + a2i * b1r + b2i)


def s5_mixer(h, h0_re, h0_im, lam_re, lam_im, log_step, b_re, b_im, c_re, c_im, d_skip, w_glu, n_blocks):
    f32 = jnp.float32
    bsz, seq, _ = h.shape
    lam_re = lam_re.astype(f32)
    lam_im = lam_im.astype(f32)
    step = jnp.exp(log_step.astype(f32))[:, None]
    mag = jnp.exp(lam_re * step)
    lbar_re = mag * jnp.cos(lam_im * step)
    lbar_im = mag * jnp.sin(lam_im * step)
    denom = lam_re * lam_re + lam_im * lam_im
    num_re = lbar_re - 1.0
    coef_re = (num_re * lam_re + lbar_im * lam_im) / denom
    coef_im = (lbar_im * lam_re - num_re * lam_im) / denom
    b_re = b_re.astype(f32)
    b_im = b_im.astype(f32)
    bbar_re = coef_re[..., None] * b_re - coef_im[..., None] * b_im
    bbar_im = coef_re[..., None] * b_im + coef_im[..., None] * b_re
    c_re = c_re.astype(f32)
    c_im = c_im.astype(f32)
    u = h.astype(f32).reshape(bsz, n_blocks, seq // n_blocks, S5_GROUPS, S5_GROUP)
    u = jnp.swapaxes(u, 0, 1)

    def block_step(carry, u_blk):
        hr, hi = carry
        bu_re = jnp.einsum('gpj,btgj->btgp', bbar_re, u_blk)
        bu_im = jnp.einsum('gpj,btgj->btgp', bbar_im, u_blk)
        bu_re = bu_re.at[:, 0].add(lbar_re * hr - lbar_im * hi)
        bu_im = bu_im.at[:, 0].add(lbar_re * hi + lbar_im * hr)
        a_re = jnp.broadcast_to(lbar_re, bu_re.shape)
        a_im = jnp.broadcast_to(lbar_im, bu_im.shape)
        _, _, xr, xi = lax.associative_scan(_complex_affine_combine, (a_re, a_im, bu_re, bu_im), axis=1)
        y = jnp.einsum('gjp,btgp->btgj', c_re, xr) - jnp.einsum('gjp,btgp->btgj', c_im, xi)
        return (xr[:, -1], xi[:, -1]), y

    (hr, hi), ys = lax.scan(block_step, (h0_re.astype(f32), h0_im.astype(f32)), u)
    y = jnp.swapaxes(ys, 0, 1).reshape(bsz, seq, D_MODEL)
    y = y + d_skip.astype(f32) * h.astype(f32)
    z = jax.nn.gelu(y).astype(h.dtype)
    val, gate = jnp.split(z @ w_glu, 2, axis=-1)
    return val * jax.nn.sigmoid(gate), hr.astype(h0_re.dtype), hi.astype(h0_im.dtype)


def band_attention(q, k, v, q_pos, k_pos, rel_bias):
    scores = jnp.einsum('bqhd,bkhd->bhqk', q, k, preferred_element_type=jnp.float32) * (ATTN_HEAD_DIM ** -0.5)
    rel = jnp.clip(q_pos[:, None] - k_pos[None, :], -REL_CLIP, REL_CLIP) + REL_CLIP
    scores = scores + rel_bias.astype(jnp.float32)[:, rel][None]
    q_chunk = q_pos // CHUNK
    k_chunk = k_pos // CHUNK
    allowed = ((k_pos[None, :] >= 0)
               & (k_chunk[None, :] <= q_chunk[:, None])
               & (k_chunk[None, :] >= q_chunk[:, None] - BAND_CHUNKS))
    scores = jnp.where(allowed[None, None], scores, -1e30)
    probs = jax.nn.softmax(scores, axis=-1).astype(v.dtype)
    return jnp.einsum('bhqk,bkhd->bqhd', probs, v)


def attn_mixer(h, cache_k, cache_v, w_qkv, rel_bias, w_o, prompt):
    bsz, seq, _ = h.shape
    qkv = (h @ w_qkv).reshape(bsz, seq, 3, ATTN_HEADS, ATTN_HEAD_DIM)
    q, k, v = qkv[:, :, 0], qkv[:, :, 1], qkv[:, :, 2]
    if prompt:
        n_chunks = seq // CHUNK
        widths = ((0, 0), (WINDOW, 0), (0, 0), (0, 0))
        k_pad = jnp.pad(k, widths)
        v_pad = jnp.pad(v, widths)

        def one_chunk(c):
            start = c * CHUNK
            q_c = lax.dynamic_slice_in_dim(q, start, CHUNK, axis=1)
            k_c = lax.dynamic_slice_in_dim(k_pad, start, WINDOW + CHUNK, axis=1)
            v_c = lax.dynamic_slice_in_dim(v_pad, start, WINDOW + CHUNK, axis=1)
            q_pos = start + jnp.arange(CHUNK)
            k_pos = start - WINDOW + jnp.arange(WINDOW + CHUNK)
            return band_attention(q_c, k_c, v_c, q_pos, k_pos, rel_bias)

        out = lax.map(one_chunk, jnp.arange(n_chunks))
        out = jnp.swapaxes(out, 0, 1).reshape(bsz, seq, D_MODEL)
        keep = min(WINDOW, seq)
        new_k = k[:, seq - keep:]
        new_v = v[:, seq - keep:]
    else:
        n_cached = cache_k.shape[1]
        k_all = jnp.concatenate([cache_k.astype(k.dtype), k], axis=1)
        v_all = jnp.concatenate([cache_v.astype(v.dtype), v], axis=1)
        q_pos = PAST_LEN + jnp.arange(seq)
        k_pos = PAST_LEN - n_cached + jnp.arange(n_cached + seq)
        out = band_attention(q, k_all, v_all, q_pos, k_pos, rel_bias).reshape(bsz, seq, D_MODEL)
        new_k = k_all[:, seq:]
        new_v = v_all[:, seq:]
    return out @ w_o, new_k, new_v


def _hgrn_block(state, blk):
    q, k, v, g = blk
    t = q.shape[1]
    b = jnp.cumsum(g, axis=1)
    o_inter = jnp.einsum('bthk,bhkv->bthv', q * jnp.exp(b), state)
    causal = jnp.tril(jnp.ones((t, t), dtype=bool))
    diff = b[:, :, None] - b[:, None, :]
    decay = jnp.exp(jnp.where(causal[None, :, :, None, None], diff, -jnp.inf))
    scores = jnp.einsum('bthk,bshk,btshk->bhts', q, k, decay)
    o_intra = jnp.einsum('bhts,bshv->bthv', scores, v)
    b_last = b[:, -1]
    new_state = (jnp.exp(b_last)[..., None] * state
                 + jnp.einsum('bshk,bshv->bhkv', k * jnp.exp(b_last[:, None] - b), v))
    return new_state, o_intra + o_inter


def hgrn_mixer(h, s0, w_in, lower_bound, norm_gain, w_o):
    f32 = jnp.float32
    bsz, seq, _ = h.shape
    q, f_logit, inp, gate = jnp.split(h @ w_in, [HGRN_FDIM, 2 * HGRN_FDIM, 2 * HGRN_FDIM + D_MODEL], axis=-1)
    lb = lower_bound.astype(f32)
    log_f = jnp.logaddexp(jnp.log(lb), jnp.log1p(-lb) + jax.nn.log_sigmoid(f_logit.astype(f32)))
    k_in = -jnp.expm1(log_f)
    q = jax.nn.silu(q.astype(f32)) * (HGRN_EXPAND ** -0.5)

    def heads(a, d):
        return a.reshape(bsz, seq, HGRN_HEADS, d)

    q = heads(q, HGRN_EXPAND)
    k_in = heads(k_in, HGRN_EXPAND)
    log_f = heads(log_f, HGRN_EXPAND)
    v = heads(inp.astype(f32), HGRN_VDIM)
    pad = (-seq) % HGRN_BLOCK
    if pad:
        widths = ((0, 0), (0, pad), (0, 0), (0, 0))
        q, k_in, v, log_f = [jnp.pad(a, widths) for a in (q, k_in, v, log_f)]
    n_blocks = (seq + pad) // HGRN_BLOCK

    def to_blocks(a):
        return jnp.swapaxes(a.reshape(bsz, n_blocks, HGRN_BLOCK, *a.shape[2:]), 0, 1)

    s_new, o = lax.scan(_hgrn_block, s0.astype(f32), (to_blocks(q), to_blocks(k_in), to_blocks(v), to_blocks(log_f)))
    o = jnp.swapaxes(o, 0, 1).reshape(bsz, seq + pad, HGRN_HEADS, HGRN_VDIM)[:, :seq]
    o = o * lax.rsqrt(jnp.mean(o * o, axis=-1, keepdims=True) + RMS_EPS)
    o = o.reshape(bsz, seq, D_MODEL) * norm_gain.astype(f32) * jax.nn.silu(gate.astype(f32))
    return o.astype(h.dtype) @ w_o, s_new.astype(s0.dtype)


def _trunk(x, prompt, s5_re0, s5_im0, attn_k0, attn_v0, hgrn0,
           norm_mixer, norm_ffn, norm_final, ffn_w_gate_up, ffn_w_down,
           s5_lambda_re, s5_lambda_im, s5_log_step, s5_b_re, s5_b_im, s5_c_re, s5_c_im, s5_d, s5_w_glu,
           attn_w_qkv, attn_rel_bias, attn_w_o,
           hgrn_w_in, hgrn_lower_bounds, hgrn_norm, hgrn_w_o):
    lbs = jnp.cumsum(jax.nn.softmax(hgrn_lower_bounds.astype(jnp.float32), axis=0), axis=0)
    lbs = lbs - lbs[0]
    n_blocks = x.shape[1] // CHUNK if prompt else 1
    new_re, new_im, new_k, new_v, new_s = [], [], [], [], []
    for layer in range(DEPTH):
        h = rms_norm(x, norm_mixer[layer])
        kind = layer % N_MIXERS
        j = layer // N_MIXERS
        if kind == 0:
            out, hr, hi = s5_mixer(h, s5_re0[j], s5_im0[j], s5_lambda_re[j], s5_lambda_im[j], s5_log_step[j],
                                   s5_b_re[j], s5_b_im[j], s5_c_re[j], s5_c_im[j], s5_d[j], s5_w_glu[j], n_blocks)
            new_re.append(hr)
            new_im.append(hi)
        elif kind == 1:
            ck = None if prompt else attn_k0[j]
            cv = None if prompt else attn_v0[j]
            out, kn, vn = attn_mixer(h, ck, cv, attn_w_qkv[j], attn_rel_bias[j], attn_w_o[j], prompt)
            new_k.append(kn)
            new_v.append(vn)
        else:
            out, sn = hgrn_mixer(h, hgrn0[j], hgrn_w_in[j], lbs[layer], hgrn_norm[j], hgrn_w_o[j])
            new_s.append(sn)
        x = x + out
        x = x + swiglu(rms_norm(x, norm_ffn[layer]), ffn_w_gate_up[layer], ffn_w_down[layer])
    y = rms_norm(x, norm_final)
    return y, jnp.stack(new_re), jnp.stack(new_im), jnp.stack(new_k), jnp.stack(new_v), jnp.stack(new_s)


def setup_inputs(seed: int = 0) -> dict:
    key = jax.random.key(seed)
    ks = jax.random.split(key, 32)
    f32 = jnp.float32

    def nrm(k, shape, scale):
        return jax.random.normal(k, shape, f32) * scale

    w_cache = min(WINDOW, PAST_LEN)
    n_idx = jnp.arange(S5_STATE, dtype=f32)
    return {
        'x_prompt': nrm(ks[0], (BATCH, SEQ, D_MODEL), 1.0),
        'x_sample': nrm(ks[1], (DEC_BATCH, DEC_SEQ, D_MODEL), 1.0),
        'state_s5_re': nrm(ks[2], (N_S5_LAYERS, DEC_BATCH, S5_GROUPS, S5_STATE), 0.5),
        'state_s5_im': nrm(ks[3], (N_S5_LAYERS, DEC_BATCH, S5_GROUPS, S5_STATE), 0.5),
        'cache_attn_k': nrm(ks[4], (N_ATTN_LAYERS, DEC_BATCH, w_cache, ATTN_HEADS, ATTN_HEAD_DIM), 1.0),
        'cache_attn_v': nrm(ks[5], (N_ATTN_LAYERS, DEC_BATCH, w_cache, ATTN_HEADS, ATTN_HEAD_DIM), 1.0),
        'state_hgrn': nrm(ks[6], (N_HGRN_LAYERS, DEC_BATCH, HGRN_HEADS, HGRN_EXPAND, HGRN_VDIM), 0.5),
        'norm_mixer': 1.0 + nrm(ks[7], (DEPTH, D_MODEL), 0.01),
        'norm_ffn': 1.0 + nrm(ks[8], (DEPTH, D_MODEL), 0.01),
        'norm_final': 1.0 + nrm(ks[9], (D_MODEL,), 0.01),
        'ffn_w_gate_up': nrm(ks[10], (DEPTH, D_MODEL, 2 * D_FF), D_MODEL ** -0.5),
        'ffn_w_down': nrm(ks[11], (DEPTH, D_FF, D_MODEL), D_FF ** -0.5),
        's5_lambda_re': -0.5 + nrm(ks[12], (N_S5_LAYERS, S5_GROUPS, S5_STATE), 0.01),
        's5_lambda_im': math.pi * n_idx + nrm(ks[13], (N_S5_LAYERS, S5_GROUPS, S5_STATE), 0.01),
        's5_log_step': jax.random.uniform(ks[14], (N_S5_LAYERS, S5_GROUPS), f32, math.log(1e-3), math.log(1e-1)),
        's5_b_re': nrm(ks[15], (N_S5_LAYERS, S5_GROUPS, S5_STATE, S5_GROUP), (2 * S5_GROUP) ** -0.5),
        's5_b_im': nrm(ks[16], (N_S5_LAYERS, S5_GROUPS, S5_STATE, S5_GROUP), (2 * S5_GROUP) ** -0.5),
        's5_c_re': nrm(ks[17], (N_S5_LAYERS, S5_GROUPS, S5_GROUP, S5_STATE), S5_STATE ** -0.5),
        's5_c_im': nrm(ks[18], (N_S5_LAYERS, S5_GROUPS, S5_GROUP, S5_STATE), S5_STATE ** -0.5),
        's5_d': nrm(ks[19], (N_S5_LAYERS, D_MODEL), 1.0),
        's5_w_glu': nrm(ks[20], (N_S5_LAYERS, D_MODEL, 2 * D_MODEL), D_MODEL ** -0.5),
        'attn_w_qkv': nrm(ks[21], (N_ATTN_LAYERS, D_MODEL, 3 * D_MODEL), D_MODEL ** -0.5),
        'attn_rel_bias': nrm(ks[22], (N_ATTN_LAYERS, ATTN_HEADS, 2 * REL_CLIP + 1), 0.1),
        'attn_w_o': nrm(ks[23], (N_ATTN_LAYERS, D_MODEL, D_MODEL), D_MODEL ** -0.5),
        'hgrn_w_in': nrm(ks[24], (N_HGRN_LAYERS, D_MODEL, 2 * HGRN_FDIM + 2 * D_MODEL), D_MODEL ** -0.5),
        'hgrn_lower_bounds': nrm(ks[25], (DEPTH, HGRN_FDIM), 0.1),
        'hgrn_norm': 1.0 + nrm(ks[26], (N_HGRN_LAYERS, D_MODEL), 0.01),
        'hgrn_w_o': nrm(ks[27], (N_HGRN_LAYERS, D_MODEL, D_MODEL), D_MODEL ** -0.5),
    }


def reference(x_prompt, x_sample, state_s5_re, state_s5_im, cache_attn_k, cache_attn_v, state_hgrn,
              norm_mixer, norm_ffn, norm_final, ffn_w_gate_up, ffn_w_down,
              s5_lambda_re, s5_lambda_im, s5_log_step, s5_b_re, s5_b_im, s5_c_re, s5_c_im, s5_d, s5_w_glu,
              attn_w_qkv, attn_rel_bias, attn_w_o,
              hgrn_w_in, hgrn_lower_bounds, hgrn_norm, hgrn_w_o):
    bsz = x_prompt.shape[0]
    zeros_s5 = jnp.zeros((N_S5_LAYERS, bsz, S5_GROUPS, S5_STATE), x_prompt.dtype)
    zeros_hgrn = jnp.zeros((N_HGRN_LAYERS, bsz, HGRN_HEADS, HGRN_EXPAND, HGRN_VDIM), x_prompt.dtype)
    y_prompt, p_s5_re, p_s5_im, p_k, p_v, p_hgrn = _trunk(
        x_prompt, True, zeros_s5, zeros_s5, None, None, zeros_hgrn,
        norm_mixer, norm_ffn, norm_final, ffn_w_gate_up, ffn_w_down,
        s5_lambda_re, s5_lambda_im, s5_log_step, s5_b_re, s5_b_im, s5_c_re, s5_c_im, s5_d, s5_w_glu,
        attn_w_qkv, attn_rel_bias, attn_w_o,
        hgrn_w_in, hgrn_lower_bounds, hgrn_norm, hgrn_w_o)
    y_sample, s_s5_re, s_s5_im, s_k, s_v, s_hgrn = _trunk(
        x_sample, False, state_s5_re, state_s5_im, cache_attn_k, cache_attn_v, state_hgrn,
        norm_mixer, norm_ffn, norm_final, ffn_w_gate_up, ffn_w_down,
        s5_lambda_re, s5_lambda_im, s5_log_step, s5_b_re, s5_b_im, s5_c_re, s5_c_im, s5_d, s5_w_glu,
        attn_w_qkv, attn_rel_bias, attn_w_o,
        hgrn_w_in, hgrn_lower_bounds, hgrn_norm, hgrn_w_o)
    return (y_prompt, y_sample, p_s5_re, p_s5_im, p_k, p_v, p_hgrn, s_s5_re, s_s5_im, s_k, s_v, s_hgrn)
```

```python
import contextlib
import math
import numpy as np
import concourse.bass as bass
import concourse.mybir as mybir
from concourse.bass_utils import run_bass_kernel_spmd

F32 = mybir.dt.float32
BF16 = mybir.dt.bfloat16
I32 = mybir.dt.int32
AF = mybir.ActivationFunctionType
ALU = mybir.AluOpType
EPS = 1e-6
TWO_PI = 2.0 * math.pi


class Cfg:
    def __init__(self, D=4096, SEQ=2048, DFF=11008, NS=2, DSEQ=32, DEPTH=4, TT=512):
        self.D, self.SEQ, self.DFF, self.NS, self.DSEQ, self.DEPTH, self.TT = D, SEQ, DFF, NS, DSEQ, DEPTH, TT
        self.NC = D // 128
        self.H = D // 128
        self.NP = D // 32
        self.NF = DFF // 128
        self.NTOK = SEQ + NS * DSEQ
        self.WIN = 512
        self.NS5 = (DEPTH + 2) // 3
        self.NAT = (DEPTH + 1) // 3
        self.NHG = DEPTH // 3


class Sem:
    def __init__(self, h, key):
        self.h, self.key = h, key


class Buf:
    __slots__ = ("ap", "w", "r", "psum")

    def __init__(self, ap=None, psum=False):
        self.ap = ap
        self.w = None
        self.r = {}
        self.psum = psum


def flat(x):
    out = []
    for b in x:
        if isinstance(b, (list, tuple)):
            out.extend(flat(b))
        elif b is not None:
            out.append(b)
    return out


class Gen:
    def __init__(self, nc, cfg):
        self.nc, self.cfg = nc, cfg
        self.es = contextlib.ExitStack()
        self.eng = {"pe": nc.tensor, "act": nc.scalar, "dve": nc.vector, "pool": nc.gpsimd, "sp": nc.sync}
        self.psem, self.pcnt = {}, {}
        for e in ("pe", "act", "dve", "pool"):
            self.psem[e] = Sem(self.es.enter_context(nc.semaphore("p_" + e)), "p_" + e)
            self.pcnt[e] = 0
        self.seen = {e: {} for e in self.eng}
        self.dsems, self.dcnt, self.dptr = {}, {}, {}
        for q in ("sp", "pool"):
            self.dsems[q] = [Sem(self.es.enter_context(nc.semaphore(f"d_{q}{i}")), f"d_{q}{i}") for i in range(16)]
            self.dcnt[q] = [0] * 16
            self.dptr[q] = 0
        self.out_toks = []
        self.nbank = 0
        self.nlb = 0

    def sb(self, name, shape, dt=F32):
        return self.es.enter_context(self.nc.sbuf_tensor(name, list(shape), dt))

    def ps(self, name, shape, dt=F32):
        return self.es.enter_context(self.nc.psum_tensor(name, list(shape), dt))

    def _deps(self, r, w, e=None):
        deps = []
        own = self.psem[e].key if e in self.psem else None
        for b in r:
            if b.w is not None:
                deps.append(b.w)
            if b.psum:
                for k, (s, v) in b.r.items():
                    if k != own:
                        deps.append((s, v))
        for b in w:
            if b.w is not None:
                deps.append(b.w)
            for k, (s, v) in b.r.items():
                deps.append((s, v))
        return deps

    def _wait(self, e, deps):
        eng = self.eng[e]
        seen = self.seen[e]
        for (s, v) in deps:
            if e == "pe" and s is self.psem["pe"]:
                continue
            if seen.get(s.key, 0) >= v:
                continue
            eng.wait_ge(s.h, v)
            seen[s.key] = v

    def _mark(self, tok, r, w):
        s, v = tok
        for b in r:
            cur = b.r.get(s.key)
            if cur is None or cur[1] < v:
                b.r[s.key] = (s, v)
        for b in w:
            b.w = tok
            b.r = {}

    def op(self, e, fn, r=(), w=()):
        r, w = flat(r), flat(w)
        self._wait(e, self._deps(r, w, e))
        ins = fn(self.eng[e])
        self.pcnt[e] += 1
        ins.then_inc(self.psem[e].h, 1)
        tok = (self.psem[e], self.pcnt[e])
        self._mark(tok, r, w)
        return tok

    def dma(self, q, out, in_, r=(), w=(), final=False):
        r, w = flat(r), flat(w)
        deps = self._deps(r, w, q)
        i = self.dptr[q] % len(self.dsems[q])
        self.dptr[q] += 1
        s = self.dsems[q][i]
        if self.dcnt[q][i] > 0:
            deps.append((s, self.dcnt[q][i]))
        self._wait(q, deps)
        ins = self.eng[q].dma_start(out=out, in_=in_)
        self.dcnt[q][i] += 16
        ins.then_inc(s.h, 16)
        tok = (s, self.dcnt[q][i])
        self._mark(tok, r, w)
        if final:
            self.out_toks.append(tok)
        return tok

    def finish(self):
        deps = list(self.out_toks)
        for q in ("sp", "pool"):
            for i, s in enumerate(self.dsems[q]):
                if self.dcnt[q][i] > 0:
                    deps.append((s, self.dcnt[q][i]))
        for e in ("pe", "act", "dve", "pool"):
            if self.pcnt[e] > 0:
                deps.append((self.psem[e], self.pcnt[e]))
        self._wait("sp", deps)
        self.es.close()

    def bank(self):
        b = self.banks[self.nbank % 6]
        self.nbank += 1
        return b

    def lbank(self):
        b = self.banks[6 + self.nlb % 2]
        self.nlb += 1
        return b


def build(cfg):
    nc = bass.Bass("TRN2", target_bir_lowering=False)
    g = Gen(nc, cfg)
    D, NC, H, NP, NF, NS, DSEQ, TT = cfg.D, cfg.NC, cfg.H, cfg.NP, cfg.NF, cfg.NS, cfg.DSEQ, cfg.TT
    NTOK, SEQ, DEPTH = cfg.NTOK, cfg.SEQ, cfg.DEPTH
    NPT = SEQ // TT
    TS = NS * DSEQ

    def din(name, shape, dt=F32):
        return nc.dram_tensor(name, list(shape), dt, kind="ExternalInput").ap()

    def dout(name, shape, dt=F32):
        return nc.dram_tensor(name, list(shape), dt, kind="ExternalOutput").ap()

    def dscr(name, shape, dt=F32):
        return nc.dram_tensor(name, list(shape), dt, kind="Internal").ap()

    xT_in = din("xT", [D, NTOK])
    c_ident = din("c_ident", [128, 128])
    c_tri = din("c_tri", [128, 64])
    c_tri32 = din("c_tri32", [128, 32])
    c_tidx = din("c_tidx", [128, 512])
    c_bm64 = din("c_bm64", [128, 512])
    c_bm32 = din("c_bm32", [128, 64])
    nm_in = din("nm", [128, DEPTH, NC])
    nf_in = din("nf", [128, DEPTH, NC])
    nfin_in = din("nfin", [128, NC])
    wgu = [din(f"wgu{l}", [2 * NF, 128, D]) for l in range(DEPTH)]
    wdn = [din(f"wdn{l}", [NC, 128, cfg.DFF]) for l in range(DEPTH)]
    s5in = []
    for j in range(cfg.NS5):
        d = {}
        for nm_ in ("lamre", "lamim", "lstep"):
            d[nm_] = din(f"s5_{nm_}{j}", [128, NP])
        for nm_ in ("bre", "bim", "cre", "cim"):
            d[nm_] = din(f"s5_{nm_}{j}", [128, NP, 16])
        d["dsk"] = din(f"s5_dsk{j}", [128, NC])
        d["wglu"] = din(f"s5_wglu{j}", [2 * NC, 128, D])
        d["st_re"] = din(f"s5_stre{j}", [NS, 128, NP])
        d["st_im"] = din(f"s5_stim{j}", [NS, 128, NP])
        s5in.append(d)
    atin = []
    for j in range(cfg.NAT):
        d = dict(wqkv=din(f"at_wqkv{j}", [3 * H, 128, D]), rb=din(f"at_rb{j}", [128, H, 257]), rbc=din(f"at_rbc{j}", [128, H]), bt=din(f"at_bt{j}", [128, H, 384]),
                 wo=din(f"at_wo{j}", [NC, 128, D]), ckT=din(f"at_ckT{j}", [NS, H, 128, 512]),
                 ck=din(f"at_ck{j}", [NS, 512, D]), cv=din(f"at_cv{j}", [NS, 512, D]))
        atin.append(d)
    hgin = []
    for j in range(cfg.NHG):
        d = dict(win=din(f"hg_win{j}", [4 * H, 128, D]), lbnd=din(f"hg_lbnd{j}", [128, DEPTH, H]),
                 hnorm=din(f"hg_norm{j}", [128, H]), wo=din(f"hg_wo{j}", [NC, 128, D]),
                 st=din(f"hg_st{j}", [NS, H, 128, 128]))
        hgin.append(d)
    yT_out = dout("yT", [D, NTOK])
    o_s5 = dout("o_s5", [cfg.NS5, 1 + NS, 2, 128, NP])
    o_kp = dout("o_kp", [max(cfg.NAT, 1), H, 128, 512])
    o_vp = dout("o_vp", [max(cfg.NAT, 1), H, 128, 512])
    o_kc = dout("o_kc", [max(cfg.NAT, 1), NS, 512 - DSEQ, D])
    o_vc = dout("o_vc", [max(cfg.NAT, 1), NS, 512 - DSEQ, D])
    o_kn = dout("o_kn", [max(cfg.NAT, 1), NS, H, 128, DSEQ])
    o_vn = dout("o_vn", [max(cfg.NAT, 1), NS, H, 128, DSEQ])
    o_hs = dout("o_hs", [max(cfg.NHG, 1), 1 + NS, H, 128, 128])
    xT = dscr("xT_scr", [D, NTOK])
    kscr = dscr("kscr", [H, 128, TT], BF16)
    vscr = dscr("vscr", [H, 128, 4, 128], BF16)
    escr = dscr("escr", [H, 128, 768])
    s5B = dscr("s5B", [NC, 128, 2, 128], BF16)
    s5C = dscr("s5C", [NC, 128, 2, 128], BF16)
    s5B_b = Buf()
    xd = [Buf() for _ in range(NPT + 1)]
    kscr_b = [Buf() for _ in range(H)]
    vscr_b = [Buf() for _ in range(H)]

    xt_t = g.sb("xt", [128, NC, TT])
    ht_t = g.sb("ht", [128, NC, TT], BF16)
    mt_t = g.sb("mt", [128, 32, TT], BF16)
    xt = [Buf(xt_t[:, c, :]) for c in range(NC)]
    ht = [Buf(ht_t[:, c, :]) for c in range(NC)]
    mt = [Buf(mt_t[:, c, :]) for c in range(32)]
    NWS = 2
    wsl_t = [g.sb(f"wsl{i}", [128, 32 * 128], BF16) for i in range(NWS)]
    wsl = [Buf(t) for t in wsl_t]
    wctr = [0]
    banks_t = [g.ps(f"bank{i}", [128, 512]) for i in range(8)]
    g.banks = [Buf(t, psum=True) for t in banks_t]
    ident_f = g.sb("ident_f", [128, 128]); ident_fb = Buf(ident_f)
    ident_b = g.sb("ident_b", [128, 128], BF16); ident_bb = Buf(ident_b)
    ones_b = g.sb("ones_b", [128, 128], BF16); ones_bb = Buf(ones_b)
    tri_b = g.sb("tri", [128, 64]); tri_bb = Buf(tri_b)
    tri32_b = g.sb("tri32", [128, 32]); tri32_bb = Buf(tri32_b)
    kc_t = g.sb("kc", [128, 576]); kc_b = Buf(kc_t)
    tidx = kc_t[:, 0:512]; tidx_b = kc_b
    bm64 = kc_t[:, 0:512]; bm64_b = kc_b
    bm32 = kc_t[:, 512:576]; bm32_b = kc_b
    zeros = g.sb("zeros", [128, 512]); zeros_b = Buf(zeros)
    nm_t = g.sb("nm_t", [128, DEPTH, NC]); nm_b = Buf(nm_t)
    nf_t = g.sb("nf_t", [128, DEPTH, NC]); nf_b = Buf(nf_t)
    nfin_t = g.sb("nfin_t", [128, NC]); nfin_b = Buf(nfin_t)
    sq_t = [g.sb(f"sq{i}", [128, TT], BF16) for i in range(2)]; sq_b = [Buf(t) for t in sq_t]
    rstd_t = g.sb("rstd", [128, TT]); rstd_b = Buf(rstd_t)
    tctr = [0]
    scr_t = g.sb("scr", [128, 8, TT])
    scr = [Buf(scr_t[:, i, :]) for i in range(8)]
    tmpA = [scr[0], scr[1]]; tmpA_t = [scr_t[:, 0, :], scr_t[:, 1, :]]
    tmpB = [scr[4], scr[5]]; tmpB_t = [scr_t[:, 4, :], scr_t[:, 5, :]]
    scrb_t = g.sb("scrb", [128, 6, TT], BF16)
    scrb = [Buf(scrb_t[:, i, :]) for i in range(6)]
    LC_BYTES = 25 * 1024
    lc_t = g.sb("lc", [128, LC_BYTES // 4])
    lc_b = Buf(lc_t)
    s5Bt = [g.sb(f"s5Bt{i}", [128, 2, 128], BF16) for i in range(2)]; s5Bt_b = [Buf(t) for t in s5Bt]
    s5Ct = [g.sb(f"s5Ct{i}", [128, 2, 128], BF16) for i in range(2)]; s5Ct_b = [Buf(t) for t in s5Ct]

    g.dma("sp", ident_f[:], c_ident[:, :], w=[ident_fb])
    g.dma("pool", ident_b[:], c_ident[:, :], w=[ident_bb])
    g.dma("sp", tri_b[:], c_tri[:, :], w=[tri_bb])
    g.dma("sp", tri32_b[:], c_tri32[:, :], w=[tri32_bb])
    g.dma("sp", nm_t[:], nm_in[:, :, :], w=[nm_b])
    g.dma("sp", nf_t[:], nf_in[:, :, :], w=[nf_b])
    g.dma("sp", nfin_t[:], nfin_in[:, :], w=[nfin_b])
    g.op("dve", lambda e: e.memset(ones_b[:], 1.0), w=[ones_bb])
    g.op("dve", lambda e: e.memset(zeros[:], 0.0), w=[zeros_b])

    def wload(wd_ap, ncols):
        s = wsl[wctr[0] % NWS]
        st = wsl_t[wctr[0] % NWS]
        wctr[0] += 1
        g.dma("pool", st[:, 0:ncols], wd_ap, w=[s])
        return s, st

    def mm_group(s, st, src, T, KC, bank):
        for kc in range(KC):
            g.op("pe", lambda e, kc=kc: e.matmul(bank.ap[:, 0:T], st[:, kc * 128:(kc + 1) * 128],
                                                  src[kc].ap[:, 0:T], start=(kc == 0), stop=(kc == KC - 1)),
                 r=[s, src[kc]], w=[bank])

    def linear(wd, m, src, T, KC=None, k0=0):
        KC = KC if KC is not None else len(src)
        s, st = wload(wd[m, :, k0 * 128:(k0 + KC) * 128], KC * 128)
        bank = g.bank()
        mm_group(s, st, src, T, KC, bank)
        return bank

    def rmsnorm(gain_t, gain_b, gsel, T, dst=None, final_out=None):
        bank = g.bank()
        for c in range(NC):
            sb_, st_ = sq_b[c % 2], sq_t[c % 2]
            g.op("act", lambda e, c=c, st_=st_: e.activation(out=st_[:, 0:T], in_=xt[c].ap[:, 0:T], func=AF.Square),
                 r=[xt[c]], w=[sb_])
            g.op("pe", lambda e, c=c, st_=st_: e.matmul(bank.ap[:, 0:T], ones_b[:], st_[:, 0:T],
                                                         start=(c == 0), stop=(c == NC - 1)),
                 r=[sb_, ones_bb], w=[bank])
        g.op("act", lambda e: e.activation(out=rstd_t[:, 0:T], in_=bank.ap[:, 0:T], func=AF.Sqrt,
                                           scale=1.0 / D, bias=eps_t[:, 0:1]), r=[bank, eps_b], w=[rstd_b])
        g.op("dve", lambda e: e.reciprocal(out=rstd_t[:, 0:T], in_=rstd_t[:, 0:T]), r=[rstd_b], w=[rstd_b])
        for c in range(NC):
            if final_out is None:
                g.op("dve", lambda e, c=c: e.scalar_tensor_tensor(out=ht[c].ap[:, 0:T], in0=xt[c].ap[:, 0:T],
                                                                   scalar=gsel(c), in1=rstd_t[:, 0:T],
                                                                   op0=ALU.mult, op1=ALU.mult),
                     r=[xt[c], rstd_b, gain_b], w=[ht[c]])
            else:
                tb, tt_ = tmpA[c % 2], tmpA_t[c % 2]
                g.op("dve", lambda e, c=c, tt_=tt_: e.scalar_tensor_tensor(out=tt_[:, 0:T], in0=xt[c].ap[:, 0:T],
                                                                            scalar=gsel(c), in1=rstd_t[:, 0:T],
                                                                            op0=ALU.mult, op1=ALU.mult),
                     r=[xt[c], rstd_b, gain_b], w=[tb])
                g.dma("sp", final_out(c), tt_[:, 0:T], r=[tb], final=True)

    eps_t = g.sb("eps_t", [128, 1]); eps_b = Buf(eps_t)
    g.op("dve", lambda e: e.memset(eps_t[:], EPS), w=[eps_b])
    pilo_t = g.sb("hpi_t", [128, 1]); pilo_b = Buf(pilo_t)
    g.op("dve", lambda e: e.memset(pilo_t[:], math.pi / 2), w=[pilo_b])

    def resid_add(c, bank, T):
        g.op("dve", lambda e: e.tensor_tensor(out=xt[c].ap[:, 0:T], in0=bank.ap[:, 0:T], in1=xt[c].ap[:, 0:T], op=ALU.add),
             r=[bank, xt[c]], w=[xt[c]])

    def ffn(l, T):
        rmsnorm(nf_t, nf_b, lambda c: nf_t[:, l, c:c + 1], T)
        nparts = (NF + 31) // 32
        base = NF // nparts
        sizes = [base + (1 if i < NF - base * nparts else 0) for i in range(nparts)]
        i0 = 0
        for sz in sizes:
            for i in range(sz):
                bg = linear(wgu[l], i0 + i, ht, T)
                bu = linear(wgu[l], NF + i0 + i, ht, T)
                k = tctr[0] % 2; tctr[0] += 1
                g.op("act", lambda e, k=k, bg=bg: e.activation(out=tmpA_t[k][:, 0:T], in_=bg.ap[:, 0:T], func=AF.Silu),
                     r=[bg], w=[tmpA[k]])
                g.op("dve", lambda e, k=k, bu=bu, i=i: e.tensor_tensor(out=mt[i].ap[:, 0:T], in0=tmpA_t[k][:, 0:T],
                                                                        in1=bu.ap[:, 0:T], op=ALU.mult),
                     r=[tmpA[k], bu], w=[mt[i]])
            for m in range(NC):
                bank = linear(wdn[l], m, mt, T, KC=sz, k0=i0)
                resid_add(m, bank, T)
            i0 += sz

    def s5_setup(j):
        d = s5in[j]
        o = 0

        def carve(n):
            nonlocal o
            ap = lc_t[:, o:o + n]
            o += n
            return ap
        th = carve(NP); rho = carve(NP); dsk = carve(NC)
        carry = carve((1 + NS) * 2 * NP)
        lc = lc_b
        g.dma("sp", kc_t[:, 0:512], c_tidx[:, :], w=[kc_b])
        sflat = xt_t[:, :, :].rearrange("p c t -> p (c t)")
        sall = xt
        assert 76 * NP <= NC * TT
        S = [sflat[:, i * NP:(i + 1) * NP] for i in range(12)]
        t_bre = sflat[:, 12 * NP:28 * NP].rearrange("p (n j) -> p n j", j=16)
        t_bim = sflat[:, 28 * NP:44 * NP].rearrange("p (n j) -> p n j", j=16)
        t_obr = sflat[:, 44 * NP:60 * NP].rearrange("p (n j) -> p n j", j=16)
        t_obi = sflat[:, 60 * NP:76 * NP].rearrange("p (n j) -> p n j", j=16)
        X_t = mt_t[:, 0:16, :].rearrange("p a t -> p (a t)").bitcast(F32)
        assert NC * 128 <= 8 * TT
        Xall = X_t[:, 0:NC * 128].rearrange("p (c m) -> p c m", m=128)
        mtb = mt[0:16]
        stgb_t = scrb_t[:, 0, :]
        stgb = scrb[0]
        lamre, lamim, step = S[0], S[1], S[2]
        g.dma("sp", lamre, d["lamre"][:, :], w=[sall])
        g.dma("sp", lamim, d["lamim"][:, :], w=[sall])
        g.dma("sp", step, d["lstep"][:, :], w=[sall])
        g.dma("sp", t_bre, d["bre"][:, :, :], w=[sall])
        g.dma("sp", t_bim, d["bim"][:, :, :], w=[sall])
        g.dma("sp", dsk, d["dsk"][:, :], w=[lc])
        V = lambda f: g.op("dve", f, r=[sall, lc, pilo_b], w=[sall, lc])
        A = lambda f: g.op("act", f, r=[sall, lc, pilo_b], w=[sall, lc])
        A(lambda e: e.activation(out=step, in_=step, func=AF.Exp))
        V(lambda e: e.tensor_tensor(out=S[3], in0=lamre, in1=step, op=ALU.mult))
        A(lambda e: e.activation(out=rho, in_=S[3], func=AF.Exp))
        V(lambda e: e.tensor_tensor(out=th, in0=lamim, in1=step, op=ALU.mult))
        V(lambda e: e.tensor_scalar(out=th, in0=th, scalar1=1.0 / TWO_PI, scalar2=None, op0=ALU.mult))
        ki_s = S[11].bitcast(I32)
        V(lambda e: e.tensor_copy(out=ki_s, in_=th))
        V(lambda e: e.tensor_tensor(out=S[4], in0=th, in1=ki_s, op=ALU.subtract))
        A(lambda e: e.activation(out=S[5], in_=S[4], func=AF.Sin, scale=TWO_PI))
        A(lambda e: e.activation(out=S[4], in_=S[4], func=AF.Abs))
        A(lambda e: e.activation(out=S[6], in_=S[4], func=AF.Sin, scale=-TWO_PI, bias=pilo_t[:, 0:1]))
        lbr, lbi = S[7], S[8]
        V(lambda e: e.tensor_tensor(out=lbr, in0=rho, in1=S[6], op=ALU.mult))
        V(lambda e: e.tensor_tensor(out=lbi, in0=rho, in1=S[5], op=ALU.mult))
        den = S[3]
        V(lambda e: e.tensor_tensor(out=S[4], in0=lamre, in1=lamre, op=ALU.mult))
        V(lambda e: e.tensor_tensor(out=den, in0=lamim, in1=lamim, op=ALU.mult))
        V(lambda e: e.tensor_tensor(out=den, in0=den, in1=S[4], op=ALU.add))
        V(lambda e: e.reciprocal(out=den, in_=den))
        nr = S[4]
        V(lambda e: e.tensor_scalar(out=nr, in0=lbr, scalar1=-1.0, scalar2=None, op0=ALU.add))
        cr, ci = S[9], S[10]
        V(lambda e: e.tensor_tensor(out=cr, in0=nr, in1=lamre, op=ALU.mult))
        V(lambda e: e.tensor_tensor(out=S[11], in0=lbi, in1=lamim, op=ALU.mult))
        V(lambda e: e.tensor_tensor(out=cr, in0=cr, in1=S[11], op=ALU.add))
        V(lambda e: e.tensor_tensor(out=cr, in0=cr, in1=den, op=ALU.mult))
        V(lambda e: e.tensor_tensor(out=ci, in0=lbi, in1=lamre, op=ALU.mult))
        V(lambda e: e.tensor_tensor(out=S[11], in0=nr, in1=lamim, op=ALU.mult))
        V(lambda e: e.tensor_tensor(out=ci, in0=ci, in1=S[11], op=ALU.subtract))
        V(lambda e: e.tensor_tensor(out=ci, in0=ci, in1=den, op=ALU.mult))
        for jj in range(16):
            V(lambda e, jj=jj: e.tensor_tensor(out=t_obr[:, :, jj], in0=t_bre[:, :, jj], in1=cr, op=ALU.mult))
            V(lambda e, jj=jj: e.tensor_tensor(out=S[11], in0=t_bim[:, :, jj], in1=ci, op=ALU.mult))
            V(lambda e, jj=jj: e.tensor_tensor(out=t_obr[:, :, jj], in0=t_obr[:, :, jj], in1=S[11], op=ALU.subtract))
            V(lambda e, jj=jj: e.tensor_tensor(out=t_obi[:, :, jj], in0=t_bim[:, :, jj], in1=cr, op=ALU.mult))
            V(lambda e, jj=jj: e.tensor_tensor(out=S[11], in0=t_bre[:, :, jj], in1=ci, op=ALU.mult))
            V(lambda e, jj=jj: e.tensor_tensor(out=t_obi[:, :, jj], in0=t_obi[:, :, jj], in1=S[11], op=ALU.add))
        for ri, src in enumerate((t_obr, t_obi)):
            g.op("dve", lambda e: e.memset(Xall, 0.0), r=[], w=[mtb])
            for g2 in range(2):
                dst = Xall[g2 * 64:(g2 + 1) * 64, :, :].rearrange("p c (q h j) -> p (c q) h j", q=4, h=2)[:, :, g2, :]
                g.op("dve", lambda e, dst=dst, src=src, g2=g2: e.tensor_copy(out=dst, in_=src[g2 * 64:(g2 + 1) * 64, :, :]),
                     r=[sall], w=[mtb])
            for c in range(NC):
                bank = g.bank()
                g.op("pe", lambda e, c=c, bank=bank: e.matmul(bank.ap[:, 0:128], Xall[:, c, :], ident_f[:], start=True, stop=True),
                     r=[mtb, ident_fb], w=[bank])
                g.op("act", lambda e, bank=bank: e.activation(out=stgb_t[:, 0:128], in_=bank.ap[:, 0:128], func=AF.Copy),
                     r=[bank], w=[stgb])
                g.dma("sp", s5B[c, :, ri, :], stgb_t[:, 0:128], r=[stgb], w=[s5B_b])
        g.dma("sp", t_bre, d["cre"][:, :, :], r=[], w=[sall])
        g.dma("sp", t_bim, d["cim"][:, :, :], r=[], w=[sall])
        for ri, (src, sgn) in enumerate(((t_bre, 1.0), (t_bim, -1.0))):
            g.op("dve", lambda e: e.memset(Xall, 0.0), r=[], w=[mtb])
            for g2 in range(2):
                dst = Xall[g2 * 64:(g2 + 1) * 64, :, :].rearrange("p c (q h j) -> p (c q) h j", q=4, h=2)[:, :, g2, :]
                g.op("dve", lambda e, dst=dst, src=src, g2=g2, sgn=sgn: e.tensor_scalar(
                    out=dst, in0=src[g2 * 64:(g2 + 1) * 64, :, :], scalar1=sgn, scalar2=None, op0=ALU.mult),
                     r=[sall], w=[mtb])
            for c in range(NC):
                g.op("act", lambda e, c=c: e.activation(out=stgb_t[:, 0:128], in_=Xall[:, c, :], func=AF.Copy), r=[mtb], w=[stgb])
                g.dma("sp", s5C[c, :, ri, :], stgb_t[:, 0:128], r=[stgb], w=[s5B_b])
        carry4 = carry.rearrange("p (s r n) -> p s r n", r=2, n=NP)
        g.op("dve", lambda e: e.memset(carry4[:, 0, :, :], 0.0), w=[lc])
        for si in range(NS):
            g.dma("sp", carry4[:, 1 + si, 0, :], d["st_re"][si, :, :], w=[lc])
            g.dma("sp", carry4[:, 1 + si, 1, :], d["st_im"][si, :, :], w=[lc])
        return dict(th=th, rho=rho, dsk=dsk, carry=carry4)

    def s5_mixer(j, K, segs, T):
        d = s5in[j]
        lc = lc_b
        for c in range(NC):
            ybank = g.lbank()
            Bc_t, Bc = s5Bt[c % 2], s5Bt_b[c % 2]
            Cc_t, Cc = s5Ct[c % 2], s5Ct_b[c % 2]
            g.dma("sp", Bc_t[:], s5B[c, :, :, :], r=[s5B_b], w=[Bc])
            g.dma("sp", Cc_t[:], s5C[c, :, :, :], r=[s5B_b], w=[Cc])
            for q in range(4):
                pr = 4 * c + q
                for (c0, L, sidx) in segs:
                    bre, bim = g.bank(), g.bank()
                    for ri, bk in ((0, bre), (1, bim)):
                        g.op("pe", lambda e, ri=ri, bk=bk: e.matmul(
                            bk.ap[:, 0:L], Bc_t[32 * q:32 * q + 32, ri, :], ht[c].ap[32 * q:32 * q + 32, c0:c0 + L],
                            start=True, stop=True, tile_position=(32 * q, 0)), r=[Bc, ht[c]], w=[bk])
                    cs, sn, ang, kib = scr[0], scr[1], scr[2], scr[3]
                    ki_t = scr_t[:, 3, :].bitcast(I32)
                    thp = K["th"][:, pr:pr + 1]
                    g.op("dve", lambda e: e.tensor_scalar(out=ki_t[:, 0:L], in0=tidx[:, 0:L], scalar1=thp, scalar2=None,
                                                          op0=ALU.mult), r=[tidx_b, lc], w=[kib])
                    g.op("dve", lambda e: e.scalar_tensor_tensor(out=ang.ap[:, 0:L], in0=tidx[:, 0:L], scalar=thp, in1=ki_t[:, 0:L],
                                                                  op0=ALU.mult, op1=ALU.subtract), r=[tidx_b, lc, kib], w=[ang])
                    g.op("act", lambda e: e.activation(out=sn.ap[:, 0:L], in_=ang.ap[:, 0:L], func=AF.Sin, scale=TWO_PI),
                         r=[ang], w=[sn])
                    g.op("act", lambda e: e.activation(out=ang.ap[:, 0:L], in_=ang.ap[:, 0:L], func=AF.Abs), r=[ang], w=[ang])
                    g.op("act", lambda e: e.activation(out=cs.ap[:, 0:L], in_=ang.ap[:, 0:L], func=AF.Sin, scale=-TWO_PI,
                                                       bias=pilo_t[:, 0:1]), r=[ang, pilo_b], w=[cs])
                    rho_bc = K["rho"][:, pr:pr + 1].to_broadcast([128, L])
                    t1, t2, rr, rim = scr[4], scr[5], scr[6], scr[7]
                    TT_ = lambda o, a, b, op, R, W: g.op("dve", lambda e: e.tensor_tensor(out=o, in0=a, in1=b, op=op), r=R, w=W)
                    TT_(t1.ap[:, 0:L], bre.ap[:, 0:L], cs.ap[:, 0:L], ALU.mult, [bre, cs], [t1])
                    TT_(t2.ap[:, 0:L], bim.ap[:, 0:L], sn.ap[:, 0:L], ALU.mult, [bim, sn], [t2])
                    TT_(rr.ap[:, 0:L], t1.ap[:, 0:L], t2.ap[:, 0:L], ALU.add, [t1, t2], [rr])
                    TT_(t1.ap[:, 0:L], bim.ap[:, 0:L], cs.ap[:, 0:L], ALU.mult, [bim, cs], [t1])
                    TT_(t2.ap[:, 0:L], bre.ap[:, 0:L], sn.ap[:, 0:L], ALU.mult, [bre, sn], [t2])
                    TT_(rim.ap[:, 0:L], t1.ap[:, 0:L], t2.ap[:, 0:L], ALU.subtract, [t1, t2], [rim])
                    car = K["carry"]
                    g.op("dve", lambda e: e.tensor_tensor_scan(out=rr.ap[:, 0:L], data0=rho_bc, data1=rr.ap[:, 0:L],
                                                               initial=car[:, sidx, 0, pr:pr + 1], op0=ALU.mult, op1=ALU.add),
                         r=[rr, lc], w=[rr])
                    g.op("dve", lambda e: e.tensor_tensor_scan(out=rim.ap[:, 0:L], data0=rho_bc, data1=rim.ap[:, 0:L],
                                                               initial=car[:, sidx, 1, pr:pr + 1], op0=ALU.mult, op1=ALU.add),
                         r=[rim, lc], w=[rim])
                    xre, xim = scrb[0], scrb[1]
                    TT_(t1.ap[:, 0:L], rr.ap[:, 0:L], cs.ap[:, 0:L], ALU.mult, [rr, cs], [t1])
                    TT_(t2.ap[:, 0:L], rim.ap[:, 0:L], sn.ap[:, 0:L], ALU.mult, [rim, sn], [t2])
                    TT_(xre.ap[:, c0:c0 + L], t1.ap[:, 0:L], t2.ap[:, 0:L], ALU.subtract, [t1, t2], [xre])
                    TT_(car[:, sidx, 0, pr:pr + 1], t1.ap[:, L - 1:L], t2.ap[:, L - 1:L], ALU.subtract, [t1, t2], [lc])
                    TT_(t1.ap[:, 0:L], rr.ap[:, 0:L], sn.ap[:, 0:L], ALU.mult, [rr, sn], [t1])
                    TT_(t2.ap[:, 0:L], rim.ap[:, 0:L], cs.ap[:, 0:L], ALU.mult, [rim, cs], [t2])
                    TT_(xim.ap[:, c0:c0 + L], t1.ap[:, 0:L], t2.ap[:, 0:L], ALU.add, [t1, t2], [xim])
                    TT_(car[:, sidx, 1, pr:pr + 1], t1.ap[:, L - 1:L], t2.ap[:, L - 1:L], ALU.add, [t1, t2], [lc])
                g.op("pe", lambda e: e.matmul(ybank.ap[32 * q:32 * q + 32, 0:T], Cc_t[:, 0, 32 * q:32 * q + 32],
                                              scrb[0].ap[:, 0:T], start=True, stop=False, tile_position=(0, 32 * q)),
                     r=[Cc, scrb[0]], w=[ybank])
                g.op("pe", lambda e: e.matmul(ybank.ap[32 * q:32 * q + 32, 0:T], Cc_t[:, 1, 32 * q:32 * q + 32],
                                              scrb[1].ap[:, 0:T], start=False, stop=True, tile_position=(0, 32 * q)),
                     r=[Cc, scrb[1]], w=[ybank])
            a, b_ = tmpB[0], tmpB[1]
            at, bt = tmpB_t[0], tmpB_t[1]
            g.op("dve", lambda e: e.scalar_tensor_tensor(out=at[:, 0:T], in0=ht[c].ap[:, 0:T], scalar=K["dsk"][:, c:c + 1],
                                                          in1=ybank.ap[:, 0:T], op0=ALU.mult, op1=ALU.add),
                 r=[ht[c], lc, ybank], w=[a])
            g.op("act", lambda e: e.activation(out=bt[:, 0:T], in_=at[:, 0:T], func=AF.Square), r=[a], w=[b_])
            g.op("dve", lambda e: e.tensor_scalar(out=bt[:, 0:T], in0=bt[:, 0:T], scalar1=0.044715, scalar2=1.0,
                                                  op0=ALU.mult, op1=ALU.add), r=[b_], w=[b_])
            g.op("dve", lambda e: e.tensor_tensor(out=bt[:, 0:T], in0=bt[:, 0:T], in1=at[:, 0:T], op=ALU.mult), r=[a, b_], w=[b_])
            g.op("act", lambda e: e.activation(out=bt[:, 0:T], in_=bt[:, 0:T], func=AF.Sigmoid, scale=2.0 * math.sqrt(2.0 / math.pi)),
                 r=[b_], w=[b_])
            g.op("dve", lambda e: e.tensor_tensor(out=mt[c].ap[:, 0:T], in0=bt[:, 0:T], in1=at[:, 0:T], op=ALU.mult),
                 r=[a, b_], w=[mt[c]])
        for m in range(NC):
            bv = linear(d["wglu"], m, mt, T, KC=NC)
            bg = linear(d["wglu"], NC + m, mt, T, KC=NC)
            k = tctr[0] % 2; tctr[0] += 1
            g.op("act", lambda e: e.activation(out=tmpA_t[k][:, 0:T], in_=bg.ap[:, 0:T], func=AF.Sigmoid), r=[bg], w=[tmpA[k]])
            g.op("dve", lambda e: e.tensor_tensor(out=tmpA_t[k][:, 0:T], in0=tmpA_t[k][:, 0:T], in1=bv.ap[:, 0:T], op=ALU.mult),
                 r=[tmpA[k], bv], w=[tmpA[k]])
            g.op("dve", lambda e: e.tensor_tensor(out=xt[m].ap[:, 0:T], in0=tmpA_t[k][:, 0:T], in1=xt[m].ap[:, 0:T], op=ALU.add),
                 r=[tmpA[k], xt[m]], w=[xt[m]])

    def s5_out(j, K):
        for s in range(1 + NS):
            for ri in range(2):
                g.dma("sp", o_s5[j, s, ri, :, :], K["carry"][:, s, ri, :], r=[lc_b], final=True)

    def at_setup(j):
        d = atin[j]
        lc = lc_b
        BT = lc_t[:, 0:H * 192].bitcast(BF16).rearrange("p (h n) -> p h n", n=384)
        cst = lc_t[:, H * 192:H * 192 + H]
        assert (H * 192 + H) * 4 <= LC_BYTES
        g.dma("sp", cst, d["rbc"][:, :], w=[lc])
        stg = scr_t[:, 0, 0:384]
        for h in range(H):
            g.dma("sp", stg, d["bt"][:, h, :], w=[scr[0]])
            g.op("dve", lambda e, h=h: e.tensor_copy(out=BT[:, h, :], in_=stg), r=[scr[0]], w=[lc])
        return dict(BT=BT, cst=cst)

    class _PT:
        pass
    PTB = _PT()
    PTB.ap = scrb_t[:, 2:4, :].rearrange("p a t -> p (a t)")
    PTB.bufs = [scrb[2], scrb[3]]

    def at_head_core(K, h, qT, qcol0, nq, kblocks, T_out_col, obank, dbank, first):
        lc = lc_b
        sA, sB = g.bank(), g.bank()
        PT = PTB
        BIDX = {0: 0, 1: 1, 4: 2}
        for (bp, kap, nk, vap, kb, vb) in kblocks:
            bk = sA if bp < 4 else sB
            col = (bp % 4) * 128
            hasb = bp in BIDX
            g.op("pe", lambda e, bk=bk, col=col, kap=kap, nk=nk, hasb=hasb: e.matmul(
                bk.ap[0:nk, col:col + nq], kap, qT.ap[:, qcol0:qcol0 + nq], start=True, stop=(not hasb)), r=[kb, qT], w=[bk])
            if hasb:
                bi = BIDX[bp]
                g.op("pe", lambda e, bk=bk, col=col, nk=nk, bi=bi: e.matmul(bk.ap[0:nk, col:col + nq], ident_b[:, 0:nk],
                                                                      K["BT"][:, h, bi * 128:bi * 128 + nq], start=False, stop=True),
                     r=[ident_bb, lc], w=[bk])
        for (bp, kap, nk, vap, kb, vb) in kblocks:
            bk = sA if bp < 4 else sB
            col = (bp % 4) * 128
            if bp in BIDX:
                g.op("act", lambda e, bk=bk, col=col, nk=nk, bp=bp: e.activation(out=PT.ap[0:nk, bp * 128:bp * 128 + nq],
                                                                           in_=bk.ap[0:nk, col:col + nq], func=AF.Exp),
                     r=[bk], w=PT.bufs)
            else:
                g.op("act", lambda e, bk=bk, col=col, nk=nk, bp=bp: e.activation(out=PT.ap[0:nk, bp * 128:bp * 128 + nq],
                                                                           in_=bk.ap[0:nk, col:col + nq], func=AF.Exp,
                                                                           bias=K["cst"][0:nk, h:h + 1]),
                     r=[bk, lc], w=PT.bufs)
        n = len(kblocks)
        for i, (bp, kap, nk, vap, kb, vb) in enumerate(kblocks):
            g.op("pe", lambda e, i=i, bp=bp, nk=nk, vap=vap: e.matmul(obank.ap[:, T_out_col:T_out_col + nq], vap,
                                                                PT.ap[0:nk, bp * 128:bp * 128 + nq], start=(i == 0), stop=(i == n - 1)),
                 r=[vb] + PT.bufs, w=[obank])
        for i, (bp, kap, nk, vap, kb, vb) in enumerate(kblocks):
            g.op("pe", lambda e, i=i, bp=bp, nk=nk: e.matmul(dbank.ap[:, T_out_col:T_out_col + nq], ones_b[0:nk, :],
                                                       PT.ap[0:nk, bp * 128:bp * 128 + nq], start=(i == 0), stop=(i == n - 1)),
                 r=[ones_bb] + PT.bufs, w=[dbank])

    def at_mixer(j, K, ti, T, is_sample):
        d = atin[j]
        Kwin_t = scrb_t[:, 4:6, :].rearrange("p a t -> p (a t)")
        Kwin = scrb[4:6]
        Vwin_t = scr_t[:, 0, :].bitcast(BF16).rearrange("p (b d) -> p b d", d=128)
        Vwin = [scr[0], scr[0]]
        stg_t = scr_t[:, 2:4, :]
        stgK, stgV = scr[2], scr[3]
        qT, vT = scrb[0], scrb[1]
        last = (not is_sample) and (ti == NPT - 1)
        for h in range(H):
            bq = linear(d["wqkv"], h, ht, T)
            bk = linear(d["wqkv"], H + h, ht, T)
            bv = linear(d["wqkv"], 2 * H + h, ht, T)
            g.op("act", lambda e: e.activation(out=qT.ap[:, 0:T], in_=bq.ap[:, 0:T], func=AF.Identity, scale=float(128 ** -0.5)),
                 r=[bq], w=[qT])
            g.op("dve", lambda e: e.tensor_copy(out=vT.ap[:, 0:T], in_=bv.ap[:, 0:T]), r=[bv], w=[vT])
            if is_sample or last:
                g.op("act", lambda e: e.activation(out=stgK.ap[:, 0:T], in_=bk.ap[:, 0:T], func=AF.Copy), r=[bk], w=[stgK])
                g.op("act", lambda e: e.activation(out=stgV.ap[:, 0:T], in_=bv.ap[:, 0:T], func=AF.Copy), r=[bv], w=[stgV])
            obank, dbank = g.lbank(), g.lbank()
            if not is_sample:
                g.op("dve", lambda e: e.tensor_copy(out=Kwin_t[:, 512:512 + T], in_=bk.ap[:, 0:T]), r=[bk], w=[Kwin[1]])
                if ti > 0:
                    g.dma("sp", Kwin_t[:, 0:512], kscr[h, :, :], r=[kscr_b[h]], w=[Kwin[0]])
                    g.dma("sp", Vwin_t[:, 0:4, :], vscr[h, :, :, :], r=[vscr_b[h]], w=[Vwin[0]])
                for blk in range(T // 128):
                    tb = g.bank()
                    g.op("pe", lambda e, blk=blk, tb=tb: e.matmul(tb.ap[:, 0:128], vT.ap[:, blk * 128:(blk + 1) * 128], ident_b[:],
                                                           start=True, stop=True), r=[vT, ident_bb], w=[tb])
                    g.op("dve", lambda e, blk=blk, tb=tb: e.tensor_copy(out=Vwin_t[:, 4 + blk, :], in_=tb.ap[:, 0:128]), r=[tb], w=[Vwin[1]])
                if last:
                    g.dma("sp", o_kp[j, h, :, :], stgK.ap[:, 0:512], r=[stgK], final=True)
                    g.dma("sp", o_vp[j, h, :, :], stgV.ap[:, 0:512], r=[stgV], final=True)
                for qb in range(T // 128):
                    kbl = []
                    for bp in range(5):
                        wb = 4 + qb - bp
                        if wb < 4 and ti == 0:
                            continue
                        kbl.append((bp, Kwin_t[:, wb * 128:(wb + 1) * 128], 128, Vwin_t[:, wb, :], Kwin[wb // 4], Vwin[wb // 4]))
                    at_head_core(K, h, qT, qb * 128, 128, kbl, qb * 128, obank, dbank, True)
                if ti < NPT - 1:
                    g.dma("sp", kscr[h, :, :], Kwin_t[:, 512:1024], r=[Kwin[1]], w=[kscr_b[h]])
                    g.dma("sp", vscr[h, :, :, :], Vwin_t[:, 4:8, :], r=[Vwin[1]], w=[vscr_b[h]])
            else:
                kTn = scr[5]
                kTn_t = scr_t[:, 5, :].bitcast(BF16)
                g.op("dve", lambda e: e.tensor_copy(out=kTn_t[:, 0:T], in_=bk.ap[:, 0:T]), r=[bk], w=[kTn])
                for si in range(NS):
                    c0 = si * DSEQ
                    g.dma("pool", Kwin_t[:, 0:512], d["ckT"][si, h, :, :], w=[Kwin[0]])
                    g.dma("pool", Vwin_t[:, 0:4, :], d["cv"][si].rearrange("(b p) (hh dd) -> p b hh dd", p=128, dd=128)[:, :, h, :],
                          w=[Vwin[0]])
                    g.op("dve", lambda e: e.tensor_copy(out=Kwin_t[:, 512:512 + DSEQ], in_=kTn_t[:, c0:c0 + DSEQ]), r=[kTn], w=[Kwin[1]])
                    tb = g.bank()
                    g.op("pe", lambda e: e.matmul(tb.ap[0:DSEQ, 0:128], vT.ap[:, c0:c0 + DSEQ], ident_b[:], start=True, stop=True),
                         r=[vT, ident_bb], w=[tb])
                    g.op("dve", lambda e: e.tensor_copy(out=Vwin_t[0:DSEQ, 4, :], in_=tb.ap[0:DSEQ, 0:128]), r=[tb], w=[Vwin[1]])
                    kbl = [(0, Kwin_t[:, 512:512 + DSEQ], DSEQ, Vwin_t[0:DSEQ, 4, :], Kwin[1], Vwin[1])]
                    for bp in range(1, 5):
                        wb = 4 - bp
                        kbl.append((bp, Kwin_t[:, wb * 128:(wb + 1) * 128], 128, Vwin_t[:, wb, :], Kwin[0], Vwin[0]))
                    at_head_core(K, h, qT, c0, DSEQ, kbl, c0, obank, dbank, True)
                    g.dma("sp", o_kn[j, si, h, :, :], stgK.ap[:, c0:c0 + DSEQ], r=[stgK], final=True)
                    g.dma("sp", o_vn[j, si, h, :, :], stgV.ap[:, c0:c0 + DSEQ], r=[stgV], final=True)
            rd = tmpB[0]
            g.op("dve", lambda e: e.reciprocal(out=tmpB_t[0][:, 0:T], in_=dbank.ap[:, 0:T]), r=[dbank], w=[rd])
            g.op("dve", lambda e: e.tensor_tensor(out=mt[h].ap[:, 0:T], in0=obank.ap[:, 0:T], in1=tmpB_t[0][:, 0:T], op=ALU.mult),
                 r=[obank, rd], w=[mt[h]])
        for m in range(NC):
            bank = linear(d["wo"], m, mt, T, KC=H)
            resid_add(m, bank, T)

    def at_cache_copy(j):
        d = atin[j]
        for si in range(NS):
            g.dma("sp", o_kc[j, si, :, :], d["ck"][si, DSEQ:512, :], final=True)
            g.dma("sp", o_vc[j, si, :, :], d["cv"][si, DSEQ:512, :], final=True)

    def hg_setup(j, l):
        d = hgin[j]
        lc = lc_b
        o = 0

        def carve(n):
            nonlocal o
            ap = lc_t[:, o:o + n]
            o += n
            return ap
        S = carve(H * 128).rearrange("p (h v) -> p h v", v=128)
        Sbf = carve(H * 64).bitcast(BF16).rearrange("p (h v) -> p h v", v=128)
        lb = carve(H); oml = carve(H); hn = carve(H)
        raw = carve(DEPTH * H).rearrange("p (l h) -> p l h", h=H)
        den = carve(H)
        g.dma("sp", raw, d["lbnd"][:, :, :], w=[lc])
        g.dma("sp", hn, d["hnorm"][:, :], w=[lc])
        g.op("act", lambda e: e.activation(out=raw, in_=raw, func=AF.Exp), r=[lc], w=[lc])
        V = lambda f: g.op("dve", f, r=[lc], w=[lc])
        V(lambda e: e.tensor_copy(out=den, in_=raw[:, 0, :]))
        for ll in range(1, DEPTH):
            V(lambda e, ll=ll: e.tensor_tensor(out=den, in0=den, in1=raw[:, ll, :], op=ALU.add))
        V(lambda e: e.memset(lb, 0.0))
        for ll in range(1, l + 1):
            V(lambda e, ll=ll: e.tensor_tensor(out=lb, in0=lb, in1=raw[:, ll, :], op=ALU.add))
        V(lambda e: e.reciprocal(out=den, in_=den))
        V(lambda e: e.tensor_tensor(out=lb, in0=lb, in1=den, op=ALU.mult))
        V(lambda e: e.tensor_scalar(out=oml, in0=lb, scalar1=-1.0, scalar2=1.0, op0=ALU.mult, op1=ALU.add))
        Sb = [Buf() for _ in range(H)]
        g.op("dve", lambda e: e.memset(S, 0.0), r=[lc], w=[lc] + Sb)
        g.op("dve", lambda e: e.memset(Sbf, 0.0), r=[lc], w=[lc] + Sb)
        g.dma("sp", kc_t[:, 0:512], c_bm64[:, :], w=[kc_b])
        g.dma("sp", kc_t[:, 512:576], c_bm32[:, :], w=[kc_b])
        return dict(S=S, Sbf=Sbf, lb=lb, oml=oml, hn=hn, Sb=Sb)

    def hg_mixer(j, K, T, segs, C, is_sample):
        d = hgin[j]
        lc = lc_b
        bm_t, bm_b = (bm32, bm32_b) if is_sample else (bm64, bm64_b)
        trm_t, trm_b = (tri32_b, tri32_bb) if is_sample else (tri_b, tri_bb)
        NB = T // C
        for h in range(H):
            Sb = K["Sb"][h]
            bq = linear(d["win"], h, ht, T)
            bf = linear(d["win"], H + h, ht, T)
            bv = linear(d["win"], 2 * H + h, ht, T)
            bgt = linear(d["win"], 3 * H + h, ht, T)
            qs, ff, lf, bb = scr[0], scr[1], scr[2], scr[3]
            gs, osb = scr[4], scr[5]
            ebl = scr[6]
            qt, kt, kh, vT, osq = scrb[0], scrb[1], scrb[2], scrb[3], scrb[4]
            Vtok_t = scrb_t[:, 5, :].rearrange("p (b d) -> p b d", d=128)
            Vtok = scrb[5]
            khtok_t = scr_t[:, 7, 0:64].bitcast(BF16)
            khtok = scr[7]
            g.op("act", lambda e: e.activation(out=qs.ap[:, 0:T], in_=bq.ap[:, 0:T], func=AF.Silu), r=[bq], w=[qs])
            g.op("act", lambda e: e.activation(out=ff.ap[:, 0:T], in_=bf.ap[:, 0:T], func=AF.Sigmoid), r=[bf], w=[ff])
            g.op("act", lambda e: e.activation(out=gs.ap[:, 0:T], in_=bgt.ap[:, 0:T], func=AF.Silu), r=[bgt], w=[gs])
            g.op("dve", lambda e: e.tensor_copy(out=vT.ap[:, 0:T], in_=bv.ap[:, 0:T]), r=[bv], w=[vT])
            g.op("dve", lambda e: e.tensor_scalar(out=ff.ap[:, 0:T], in0=ff.ap[:, 0:T], scalar1=K["oml"][:, h:h + 1],
                                                  scalar2=K["lb"][:, h:h + 1], op0=ALU.mult, op1=ALU.add), r=[ff, lc], w=[ff])
            g.op("act", lambda e: e.activation(out=lf.ap[:, 0:T], in_=ff.ap[:, 0:T], func=AF.Ln), r=[ff], w=[lf])
            g.op("dve", lambda e: e.tensor_scalar(out=ff.ap[:, 0:T], in0=ff.ap[:, 0:T], scalar1=-1.0, scalar2=1.0,
                                                  op0=ALU.mult, op1=ALU.add), r=[ff], w=[ff])
            g.op("dve", lambda e: e.tensor_tensor_scan(out=bb.ap[:, 0:T], data0=bm_t[:, 0:T], data1=lf.ap[:, 0:T], initial=0.0,
                                                       op0=ALU.mult, op1=ALU.add), r=[bm_b, lf], w=[bb])
            g.op("act", lambda e: e.activation(out=lf.ap[:, 0:T], in_=bb.ap[:, 0:T], func=AF.Exp), r=[bb], w=[lf])
            g.op("dve", lambda e: e.scalar_tensor_tensor(out=qt.ap[:, 0:T], in0=qs.ap[:, 0:T], scalar=float(128 ** -0.5),
                                                          in1=lf.ap[:, 0:T], op0=ALU.mult, op1=ALU.mult), r=[qs, lf], w=[qt])
            bbv = bb.ap[:, 0:T].rearrange("p (n c) -> p n c", c=C)[:, :, C - 1]
            g.op("act", lambda e: e.activation(out=ebl.ap[:, 0:NB], in_=bbv, func=AF.Exp), r=[bb], w=[ebl])
            g.op("act", lambda e: e.activation(out=lf.ap[:, 0:T], in_=bb.ap[:, 0:T], func=AF.Exp, scale=-1.0), r=[bb], w=[lf])
            g.op("dve", lambda e: e.tensor_tensor(out=kt.ap[:, 0:T], in0=ff.ap[:, 0:T], in1=lf.ap[:, 0:T], op=ALU.mult), r=[ff, lf], w=[kt])
            for blk in range(NB):
                g.op("dve", lambda e, blk=blk: e.tensor_scalar(out=kh.ap[:, blk * C:(blk + 1) * C], in0=kt.ap[:, blk * C:(blk + 1) * C],
                                                               scalar1=ebl.ap[:, blk:blk + 1], scalar2=None, op0=ALU.mult),
                     r=[kt, ebl], w=[kh])
            for b128 in range((T + 127) // 128):
                n = min(128, T - b128 * 128)
                tb = g.bank()
                g.op("pe", lambda e, b128=b128, n=n, tb=tb: e.matmul(tb.ap[0:n, 0:128], vT.ap[:, b128 * 128:b128 * 128 + n], ident_b[:],
                                                              start=True, stop=True), r=[vT, ident_bb], w=[tb])
                g.op("dve", lambda e, b128=b128, n=n, tb=tb: e.tensor_copy(out=Vtok_t[0:n, b128, :], in_=tb.ap[0:n, 0:128]), r=[tb], w=[Vtok])
            obank = g.lbank()
            for (c0, nblk, sidx) in segs:
                if sidx is not None:
                    g.dma("sp", K["S"][:, h, :], d["st"][sidx, h, :, :], w=[Sb])
                    g.op("act", lambda e: e.activation(out=K["Sbf"][:, h, :], in_=K["S"][:, h, :], func=AF.Copy), r=[Sb], w=[Sb])
                for bi in range(nblk):
                    t0 = c0 + bi * C
                    pb = t0 % 128
                    vblk = Vtok_t[pb:pb + C, t0 // 128, :]
                    sbk = g.bank()
                    g.op("pe", lambda e: e.matmul(sbk.ap[pb:pb + C, 0:C], kt.ap[:, t0:t0 + C], qt.ap[:, t0:t0 + C], start=True, stop=True,
                                                  tile_position=(0, pb)), r=[kt, qt], w=[sbk])
                    sc = scrb[4]
                    g.op("dve", lambda e: e.tensor_tensor(out=sc.ap[pb:pb + C, 0:C], in0=sbk.ap[pb:pb + C, 0:C], in1=trm_t[pb:pb + C, 0:C],
                                                          op=ALU.mult), r=[sbk, trm_b], w=[sc])
                    g.op("pe", lambda e: e.matmul(obank.ap[:, t0:t0 + C], vblk, sc.ap[pb:pb + C, 0:C], start=True, stop=False),
                         r=[Vtok, sc], w=[obank])
                    g.op("pe", lambda e: e.matmul(obank.ap[:, t0:t0 + C], K["Sbf"][:, h, :], qt.ap[:, t0:t0 + C], start=False, stop=True),
                         r=[Sb, qt], w=[obank])
                    tb = g.bank()
                    g.op("pe", lambda e: e.matmul(tb.ap[pb:pb + C, 0:128], kh.ap[:, t0:t0 + C], ident_b[:], start=True, stop=True,
                                                  tile_position=(0, pb)), r=[kh, ident_bb], w=[tb])
                    g.op("act", lambda e: e.activation(out=khtok_t[pb:pb + C, :], in_=tb.ap[pb:pb + C, 0:128], func=AF.Copy), r=[tb], w=[khtok])
                    ub = g.bank()
                    g.op("pe", lambda e: e.matmul(ub.ap[:, 0:128], khtok_t[pb:pb + C, :], vblk, start=True, stop=True), r=[khtok, Vtok], w=[ub])
                    blk = t0 // C
                    g.op("dve", lambda e: e.scalar_tensor_tensor(out=K["S"][:, h, :], in0=K["S"][:, h, :], scalar=ebl.ap[:, blk:blk + 1],
                                                                  in1=ub.ap[:, 0:128], op0=ALU.mult, op1=ALU.add), r=[Sb, ebl, ub], w=[Sb])
                    g.op("act", lambda e: e.activation(out=K["Sbf"][:, h, :], in_=K["S"][:, h, :], func=AF.Copy), r=[Sb], w=[Sb])
                if sidx is not None:
                    g.dma("sp", o_hs[j, 1 + sidx, h, :, :], K["S"][:, h, :], r=[Sb], final=True)
            g.op("act", lambda e: e.activation(out=osb.ap[:, 0:T], in_=obank.ap[:, 0:T], func=AF.Copy), r=[obank], w=[osb])
            g.op("act", lambda e: e.activation(out=osq.ap[:, 0:T], in_=obank.ap[:, 0:T], func=AF.Square), r=[obank], w=[osq])
            nb_ = g.bank()
            g.op("pe", lambda e: e.matmul(nb_.ap[:, 0:T], ones_b[:], osq.ap[:, 0:T], start=True, stop=True), r=[ones_bb, osq], w=[nb_])
            g.op("act", lambda e: e.activation(out=lf.ap[:, 0:T], in_=nb_.ap[:, 0:T], func=AF.Sqrt, scale=1.0 / 128, bias=eps_t[:, 0:1]),
                 r=[nb_, eps_b], w=[lf])
            g.op("dve", lambda e: e.reciprocal(out=lf.ap[:, 0:T], in_=lf.ap[:, 0:T]), r=[lf], w=[lf])
            g.op("dve", lambda e: e.scalar_tensor_tensor(out=osb.ap[:, 0:T], in0=osb.ap[:, 0:T], scalar=K["hn"][:, h:h + 1],
                                                          in1=lf.ap[:, 0:T], op0=ALU.mult, op1=ALU.mult), r=[osb, lc, lf], w=[osb])
            g.op("dve", lambda e: e.tensor_tensor(out=mt[h].ap[:, 0:T], in0=osb.ap[:, 0:T], in1=gs.ap[:, 0:T], op=ALU.mult),
                 r=[osb, gs], w=[mt[h]])
        for m in range(NC):
            bank = linear(d["wo"], m, mt, T, KC=H)
            resid_add(m, bank, T)

    def hg_out_prompt(j, K):
        for h in range(H):
            g.dma("sp", o_hs[j, 0, h, :, :], K["S"][:, h, :], r=[K["Sb"][h]], final=True)

    tiles = [(False, ti, ti * TT, TT) for ti in range(NPT)] + [(True, NPT, SEQ, TS)]
    xTv_in = xT_in.rearrange("(c p) t -> p c t", p=128)
    xTv = xT.rearrange("(c p) t -> p c t", p=128)
    yTv = yT_out.rearrange("(c p) t -> p c t", p=128)
    for l in range(DEPTH):
        kind, j = l % 3, l // 3
        if kind == 0:
            K = s5_setup(j)
        elif kind == 1:
            K = at_setup(j)
            at_cache_copy(j)
        else:
            K = hg_setup(j, l)
        for (is_s, ti, tok0, T) in tiles:
            src = xTv_in if l == 0 else xTv
            cg = max(1, NC // 4)
            for c0 in range(0, NC, cg):
                g.dma("sp", xt_t[:, c0:c0 + cg, 0:T], src[:, c0:c0 + cg, tok0:tok0 + T],
                      r=([] if l == 0 else [xd[ti]]), w=xt[c0:c0 + cg])
            rmsnorm(nm_t, nm_b, lambda c: nm_t[:, l, c:c + 1], T)
            if kind == 0:
                segs = [(0, T, 0)] if not is_s else [(si * DSEQ, DSEQ, 1 + si) for si in range(NS)]
                s5_mixer(j, K, segs, T)
            elif kind == 1:
                at_mixer(j, K, ti, T, is_s)
            else:
                if not is_s:
                    hg_mixer(j, K, T, [(0, T // 64, None)], 64, False)
                    if ti == NPT - 1:
                        hg_out_prompt(j, K)
                else:
                    hg_mixer(j, K, T, [(si * DSEQ, 1, si) for si in range(NS)], DSEQ, True)
            ffn(l, T)
            if l < DEPTH - 1:
                g.dma("sp", xTv[:, :, tok0:tok0 + T], xt_t[:, :, 0:T], r=xt, w=[xd[ti]])
            else:
                rmsnorm(nfin_t, nfin_b, lambda c: nfin_t[:, c:c + 1], T,
                        final_out=lambda c: yTv[:, c, tok0:tok0 + T])
        if kind == 0:
            s5_out(j, K)
    g.finish()
    return nc


def tile_w(W):
    K, N = W.shape
    return np.ascontiguousarray(W.reshape(K // 128, 128, N // 128, 128).transpose(2, 1, 0, 3)).reshape(N // 128, 128, K)


def pair_layout(a):
    G = a.shape[0]
    r = a.reshape(G // 2, 2, 64, *a.shape[2:])
    perm = (1, 2, 0) + tuple(range(3, r.ndim))
    return np.ascontiguousarray(r.transpose(perm)).reshape(128, G // 2, *a.shape[2:])


def feat_layout(v, D):
    nc_ = D // 128
    r = v.reshape(*v.shape[:-1], nc_, 128)
    return np.ascontiguousarray(np.moveaxis(r, -1, 0))


def consts():
    ident = np.eye(128, dtype=np.float32)
    p = np.arange(128)[:, None] % 64
    t = np.arange(64)[None, :]
    tri = (p <= t).astype(np.float32)
    tri32 = ((np.arange(128)[:, None] % 32) <= np.arange(32)[None, :]).astype(np.float32)
    tidx = np.broadcast_to(np.arange(1, 513, dtype=np.float32)[None, :], (128, 512)).copy()
    bm64 = np.ones((128, 512), np.float32); bm64[:, ::64] = 0
    bm32 = np.ones((128, 64), np.float32); bm32[:, ::32] = 0
    return dict(c_ident=ident, c_tri=tri, c_tri32=tri32, c_tidx=tidx, c_bm64=bm64, c_bm32=bm32)


def make_in_maps(cfg, inp, n_cores):
    D, NS, DEPTH = cfg.D, cfg.NS, cfg.DEPTH
    f = lambda a: np.asarray(a, dtype=np.float32)
    shared = dict(consts())
    shared["nm"] = feat_layout(f(inp["norm_mixer"]), D)
    shared["nf"] = feat_layout(f(inp["norm_ffn"]), D)
    shared["nfin"] = feat_layout(f(inp["norm_final"]), D)
    for l in range(DEPTH):
        shared[f"wgu{l}"] = tile_w(f(inp["ffn_w_gate_up"][l]))
        shared[f"wdn{l}"] = tile_w(f(inp["ffn_w_down"][l]))
    for j in range(cfg.NS5):
        shared[f"s5_lamre{j}"] = pair_layout(f(inp["s5_lambda_re"][j]))
        shared[f"s5_lamim{j}"] = pair_layout(f(inp["s5_lambda_im"][j]))
        shared[f"s5_lstep{j}"] = pair_layout(np.repeat(f(inp["s5_log_step"][j])[:, None], 64, axis=1))
        shared[f"s5_bre{j}"] = pair_layout(f(inp["s5_b_re"][j]))
        shared[f"s5_bim{j}"] = pair_layout(f(inp["s5_b_im"][j]))
        shared[f"s5_cre{j}"] = pair_layout(f(inp["s5_c_re"][j]).transpose(0, 2, 1))
        shared[f"s5_cim{j}"] = pair_layout(f(inp["s5_c_im"][j]).transpose(0, 2, 1))
        shared[f"s5_dsk{j}"] = feat_layout(f(inp["s5_d"][j]), D)
        shared[f"s5_wglu{j}"] = tile_w(f(inp["s5_w_glu"][j]))
    for j in range(cfg.NAT):
        shared[f"at_wqkv{j}"] = tile_w(f(inp["attn_w_qkv"][j]))
        shared[f"at_wo{j}"] = tile_w(f(inp["attn_w_o"][j]))
        shared[f"at_rb{j}"] = np.ascontiguousarray(np.broadcast_to(f(inp["attn_rel_bias"][j])[None], (128, cfg.H, 257)))
        shared[f"at_rbc{j}"] = np.ascontiguousarray(shared[f"at_rb{j}"][:, :, 256])
        rb = f(inp["attn_rel_bias"][j])
        idx = np.clip(np.arange(640)[None, :] - np.arange(128)[:, None], -128, 128) + 128
        Tm = rb[:, idx]
        Tm[:, 64:128, 0:64] = -30000.0
        Tm[:, 0:64, 576:640] = -30000.0
        shared[f"at_bt{j}"] = np.ascontiguousarray(np.concatenate([Tm[:, :, 0:256], Tm[:, :, 512:640]], axis=2).transpose(1, 0, 2))
    for j in range(cfg.NHG):
        shared[f"hg_win{j}"] = tile_w(f(inp["hgrn_w_in"][j]))
        shared[f"hg_wo{j}"] = tile_w(f(inp["hgrn_w_o"][j]))
        lb = f(inp["hgrn_lower_bounds"])
        shared[f"hg_lbnd{j}"] = np.ascontiguousarray(lb.reshape(DEPTH, cfg.H, 128).transpose(2, 0, 1))
        shared[f"hg_norm{j}"] = np.ascontiguousarray(f(inp["hgrn_norm"][j]).reshape(cfg.H, 128).T)
    maps = []
    xp, xs = f(inp["x_prompt"]), f(inp["x_sample"])
    for c in range(n_cores):
        m = dict(shared)
        xcat = np.concatenate([xp[c]] + [xs[c * NS + s] for s in range(NS)], axis=0)
        m["xT"] = np.ascontiguousarray(xcat.T)
        sl = slice(c * NS, (c + 1) * NS)
        for j in range(cfg.NS5):
            m[f"s5_stre{j}"] = np.stack([pair_layout(f(inp["state_s5_re"][j][c * NS + s])) for s in range(NS)])
            m[f"s5_stim{j}"] = np.stack([pair_layout(f(inp["state_s5_im"][j][c * NS + s])) for s in range(NS)])
        for j in range(cfg.NAT):
            ck = f(inp["cache_attn_k"][j][sl])
            cv = f(inp["cache_attn_v"][j][sl])
            m[f"at_ckT{j}"] = np.ascontiguousarray(ck.transpose(0, 2, 3, 1))
            m[f"at_ck{j}"] = np.ascontiguousarray(ck.reshape(NS, 512, D))
            m[f"at_cv{j}"] = np.ascontiguousarray(cv.reshape(NS, 512, D))
        for j in range(cfg.NHG):
            m[f"hg_st{j}"] = np.ascontiguousarray(f(inp["state_hgrn"][j][sl]))
        maps.append(m)
    return maps


def unpair(a):
    NP = a.shape[1]
    return np.ascontiguousarray(a.reshape(2, 64, NP).transpose(2, 0, 1)).reshape(NP * 2, 64)


def assemble(cfg, results, n_cores):
    D, NS, SEQ, DSEQ, H = cfg.D, cfg.NS, cfg.SEQ, cfg.DSEQ, cfg.H
    yp, ys = [], []
    s5rp, s5ip, s5rs, s5is = [], [], [], []
    kp, vp, ks, vs, hp, hs = [], [], [], [], [], []
    for c in range(n_cores):
        r = results[c]
        y = r["yT"].T
        yp.append(y[:SEQ])
        for s in range(NS):
            ys.append(y[SEQ + s * DSEQ: SEQ + (s + 1) * DSEQ])
        o = r["o_s5"]
        s5rp.append(np.stack([unpair(o[j, 0, 0]) for j in range(cfg.NS5)]))
        s5ip.append(np.stack([unpair(o[j, 0, 1]) for j in range(cfg.NS5)]))
        for s in range(NS):
            s5rs.append(np.stack([unpair(o[j, 1 + s, 0]) for j in range(cfg.NS5)]))
            s5is.append(np.stack([unpair(o[j, 1 + s, 1]) for j in range(cfg.NS5)]))
        kp.append(r["o_kp"][:cfg.NAT].transpose(0, 3, 1, 2))
        vp.append(r["o_vp"][:cfg.NAT].transpose(0, 3, 1, 2))
        for s in range(NS):
            kc = r["o_kc"][:cfg.NAT, s].reshape(cfg.NAT, 512 - DSEQ, H, 128)
            vc = r["o_vc"][:cfg.NAT, s].reshape(cfg.NAT, 512 - DSEQ, H, 128)
            kn = r["o_kn"][:cfg.NAT, s].transpose(0, 3, 1, 2)
            vn = r["o_vn"][:cfg.NAT, s].transpose(0, 3, 1, 2)
            ks.append(np.concatenate([kc, kn], axis=1))
            vs.append(np.concatenate([vc, vn], axis=1))
        hp.append(r["o_hs"][:cfg.NHG, 0])
        for s in range(NS):
            hs.append(r["o_hs"][:cfg.NHG, 1 + s])
    st = lambda lst: np.ascontiguousarray(np.stack(lst, axis=1), dtype=np.float32)
    return (np.ascontiguousarray(np.stack(yp), dtype=np.float32), np.ascontiguousarray(np.stack(ys), dtype=np.float32),
            st(s5rp), st(s5ip), st(kp), st(vp), st(hp), st(s5rs), st(s5is), st(ks), st(vs), st(hs))


def run(cfg, inputs, n_cores):
    nc = build(cfg)
    maps = make_in_maps(cfg, inputs, n_cores)
    res = run_bass_kernel_spmd(nc, maps, core_ids=list(range(n_cores)))
    return assemble(cfg, res.results, n_cores)


def kernel(**inputs):
    cfg = Cfg()
    return run(cfg, inputs, 8)
```
